# Optimizing a Trainium2 kernel written in Bass

```python
import jax, jax.numpy as jnp
from jax import lax
import numpy as np

D_MODEL = 4096
BATCH = 2
SEQ = 4096
DEPTH = 2

CHUNK = 64
Q_BLOCK = 128
SB_HEADS = 16
SB_HEAD_DIM = 128
MLA_HEADS = 16
MLA_NOPE_DIM = 128
MLA_ROPE_DIM = 64
MLA_V_DIM = 128
MLA_Q_RANK = 1024
MLA_KV_RANK = 512
D_FF = 4 * D_MODEL
ROPE_THETA = 10000.0
EPS = 1e-6
SB_WIDTH = SB_HEADS * SB_HEAD_DIM
MLA_WIDTH = MLA_HEADS * MLA_V_DIM
IN_WIDTH = 3 * SB_WIDTH + MLA_Q_RANK + MLA_KV_RANK + MLA_ROPE_DIM + 2 * D_MODEL

kernel_name = "hybrid_stickbreak_mla_block"


def _in_split_points():
    sizes = [SB_WIDTH, SB_WIDTH, SB_WIDTH, MLA_Q_RANK, MLA_KV_RANK + MLA_ROPE_DIM, D_MODEL, D_MODEL]
    pts, acc = [], 0
    for s in sizes[:-1]:
        acc += s
        pts.append(acc)
    return pts


def rms_norm(x, g):
    xf = x.astype(jnp.float32)
    y = xf * lax.rsqrt(jnp.mean(xf * xf, axis=-1, keepdims=True) + EPS)
    return (y * g.astype(jnp.float32)).astype(x.dtype)


def modulate(h, shift, scale):
    return h * (1.0 + scale[:, None, :]) + shift[:, None, :]


def rope_tables(positions, dtype):
    half = MLA_ROPE_DIM // 2
    inv_freq = ROPE_THETA ** (-jnp.arange(half, dtype=jnp.float32) / half)
    ang = positions.astype(jnp.float32)[..., None] * inv_freq
    return jnp.cos(ang).astype(dtype), jnp.sin(ang).astype(dtype)


def apply_rope(x, cos, sin):
    x1, x2 = jnp.split(x, 2, axis=-1)
    return jnp.concatenate([x1 * cos - x2 * sin, x2 * cos + x1 * sin], axis=-1)


def stick_breaking_attention(q, k, v):
    B, S, H, Dh = q.shape
    nb = S // Q_BLOCK
    scale = Dh ** -0.5
    q_blocks = q.reshape(B, nb, Q_BLOCK, H, Dh).swapaxes(0, 1)
    key_pos = jnp.arange(S)

    def block(args):
        q_blk, i = args
        z = jnp.einsum('bqhd,bkhd->bhqk', q_blk, k, preferred_element_type=jnp.float32) * scale
        q_pos = i * Q_BLOCK + jnp.arange(Q_BLOCK)
        strict = key_pos[None, :] < q_pos[:, None]
        log_fail = jnp.where(strict, jax.nn.log_sigmoid(-z), 0.0)
        after = lax.cumsum(log_fail, axis=3, reverse=True) - log_fail
        w = jnp.where(strict, jnp.exp(jax.nn.log_sigmoid(z) + after), 0.0)
        return jnp.einsum('bhqk,bkhd->bqhd', w, v.astype(jnp.float32))

    out = lax.map(block, (q_blocks, jnp.arange(nb)))
    return out.swapaxes(0, 1).reshape(B, S, H * Dh).astype(q.dtype)


def latent_attention(q_nope, q_rope, k_nope, k_rope, v):
    B, S, H, Dn = q_nope.shape
    Dv = v.shape[-1]
    nb = S // Q_BLOCK
    scale = (Dn + q_rope.shape[-1]) ** -0.5
    qn_b = q_nope.reshape(B, nb, Q_BLOCK, H, Dn).swapaxes(0, 1)
    qr_b = q_rope.reshape(B, nb, Q_BLOCK, H, -1).swapaxes(0, 1)
    key_chunk = jnp.arange(S) // CHUNK

    def block(args):
        qn, qr, i = args
        s = (jnp.einsum('bqhd,bkhd->bhqk', qn, k_nope, preferred_element_type=jnp.float32)
             + jnp.einsum('bqhr,bkr->bhqk', qr, k_rope, preferred_element_type=jnp.float32)) * scale
        q_chunk = (i * Q_BLOCK + jnp.arange(Q_BLOCK)) // CHUNK
        mask = key_chunk[None, :] <= q_chunk[:, None]
        p = jax.nn.softmax(jnp.where(mask, s, -jnp.inf), axis=-1)
        return jnp.einsum('bhqk,bkhd->bqhd', p, v.astype(jnp.float32))

    out = lax.map(block, (qn_b, qr_b, jnp.arange(nb)))
    return out.swapaxes(0, 1).reshape(B, S, H * Dv).astype(v.dtype)


def setup_inputs(seed: int = 0) -> dict:
    key = jax.random.key(seed)
    ks = jax.random.split(key, 24)
    L = DEPTH

    def nrm(k, shape, scale):
        return jax.random.normal(k, shape, jnp.float32) * scale

    def gain(k, n):
        return 1.0 + 0.1 * nrm(k, (L, n), 1.0)

    x = nrm(ks[0], (BATCH, SEQ, D_MODEL), 1.0)
    c = nrm(ks[1], (BATCH, D_MODEL), 1.0)
    offset = jax.random.randint(ks[2], (BATCH, 1), 0, 4096)
    positions = (offset + jnp.arange(SEQ)[None, :]).astype(jnp.int32)
    return {
        "x": x,
        "c": c,
        "positions": positions,
        "w_ada": nrm(ks[3], (L, D_MODEL, 6 * D_MODEL), 0.5 * D_MODEL ** -0.5),
        "b_ada": nrm(ks[4], (L, 6 * D_MODEL), 0.02),
        "g_pre_mix": gain(ks[5], D_MODEL),
        "g_post_mix": gain(ks[6], D_MODEL),
        "g_pre_mlp": gain(ks[7], D_MODEL),
        "g_post_mlp": gain(ks[8], D_MODEL),
        "w_in": nrm(ks[9], (L, D_MODEL, IN_WIDTH), D_MODEL ** -0.5),
        "g_q_lora": gain(ks[10], MLA_Q_RANK),
        "w_uq": nrm(ks[11], (L, MLA_Q_RANK, MLA_HEADS * (MLA_NOPE_DIM + MLA_ROPE_DIM)), MLA_Q_RANK ** -0.5),
        "g_kv_lora": gain(ks[12], MLA_KV_RANK),
        "w_ukv": nrm(ks[13], (L, MLA_KV_RANK, MLA_HEADS * (MLA_NOPE_DIM + MLA_V_DIM)), MLA_KV_RANK ** -0.5),
        "w_o_sb": nrm(ks[14], (L, SB_WIDTH, D_MODEL), SB_WIDTH ** -0.5),
        "w_o_mla": nrm(ks[15], (L, MLA_WIDTH, D_MODEL), MLA_WIDTH ** -0.5),
        "w_out": nrm(ks[16], (L, D_MODEL, D_MODEL), D_MODEL ** -0.5),
        "w_up": nrm(ks[17], (L, D_MODEL, D_FF), D_MODEL ** -0.5),
        "w_down": nrm(ks[18], (L, D_FF, D_MODEL), D_FF ** -0.5),
    }


def reference(x, c, positions, w_ada, b_ada, g_pre_mix, g_post_mix, g_pre_mlp, g_post_mlp, w_in,
              g_q_lora, w_uq, g_kv_lora, w_ukv, w_o_sb, w_o_mla, w_out, w_up, w_down):
    B, S, _ = x.shape
    cos, sin = rope_tables(positions, x.dtype)
    split_pts = _in_split_points()
    c_act = jax.nn.silu(c)
    for l in range(DEPTH):
        ada = c_act @ w_ada[l] + b_ada[l]
        sh1, sc1, gt1, sh2, sc2, gt2 = jnp.split(ada, 6, axis=-1)

        h = modulate(rms_norm(x, g_pre_mix[l]), sh1, sc1)
        proj = h @ w_in[l]
        q_sb, k_sb, v_sb, q_down, kv_down, gl_sb, gl_mla = jnp.split(proj, split_pts, axis=-1)

        o_sb = stick_breaking_attention(
            q_sb.reshape(B, S, SB_HEADS, SB_HEAD_DIM),
            k_sb.reshape(B, S, SB_HEADS, SB_HEAD_DIM),
            v_sb.reshape(B, S, SB_HEADS, SB_HEAD_DIM))
        br_sb = o_sb @ w_o_sb[l]

        c_q = rms_norm(q_down, g_q_lora[l])
        q = (c_q @ w_uq[l]).reshape(B, S, MLA_HEADS, MLA_NOPE_DIM + MLA_ROPE_DIM)
        q_nope, q_rope = q[..., :MLA_NOPE_DIM], q[..., MLA_NOPE_DIM:]
        q_rope = apply_rope(q_rope, cos[:, :, None, :], sin[:, :, None, :])
        c_kv = rms_norm(kv_down[..., :MLA_KV_RANK], g_kv_lora[l])
        k_rope = apply_rope(kv_down[..., MLA_KV_RANK:], cos, sin)
        kv = (c_kv @ w_ukv[l]).reshape(B, S, MLA_HEADS, MLA_NOPE_DIM + MLA_V_DIM)
        k_nope, v_mla = kv[..., :MLA_NOPE_DIM], kv[..., MLA_NOPE_DIM:]
        o_mla = latent_attention(q_nope, q_rope, k_nope, k_rope, v_mla)
        br_mla = o_mla @ w_o_mla[l]

        merged = jax.nn.sigmoid(gl_sb) * br_sb + jax.nn.sigmoid(gl_mla) * br_mla
        y = merged @ w_out[l]
        x = x + gt1[:, None, :] * rms_norm(y, g_post_mix[l])

        h = modulate(rms_norm(x, g_pre_mlp[l]), sh2, sc2)
        y = jnp.square(jax.nn.relu(h @ w_up[l])) @ w_down[l]
        x = x + gt2[:, None, :] * rms_norm(y, g_post_mlp[l])
    return x
```

```python
import numpy as np
import ml_dtypes
import concourse.bass as bass
import concourse.mybir as mybir
from concourse.bass_utils import run_bass_kernel_spmd

F32 = mybir.dt.float32
BF16 = mybir.dt.bfloat16
I32 = mybir.dt.int32
AF = mybir.ActivationFunctionType
ALU = mybir.AluOpType

NCORES = 8
EPS = 1e-6
NEG = -30000.0


class Cfg:
    def __init__(s, D=4096, S=4096, QR=1024, KVR=512):
        s.B = 2
        s.D = D
        s.S = S
        s.QR = QR
        s.KVR = KVR
        s.L = 2
        s.DFF = 4 * D
        s.DC = D // 128
        s.TOK = s.B * S // NCORES
        s.TT = min(512, s.TOK)
        s.NTT = s.TOK // s.TT
        s.NTOK = s.B * S
        s.SBW = 2048
        s.ACOLS = 6 * D // NCORES
        s.HB = s.DFF // NCORES
        s.HBC = s.HB // 128
        s.CTU = min(256, s.HB)
        s.QT = 512
        s.NQT = S // 512
        s.INW = 3 * 2048 + QR + KVR + 64 + 2 * D


SAME_ENGINE_RAW_SYNC = True


class Buf:
    __slots__ = ("name", "w", "r")

    def __init__(self, name=""):
        self.name = name
        self.w = {}
        self.r = {}


class Eng:
    def __init__(self, ctx, name, h, compute):
        self.ctx = ctx
        self.name = name
        self.h = h
        self.compute = compute
        self.sem = ctx.new_sem()
        self.count = 0
        self.seen = {}
        self.slots = []
        self.rr = 0

    def wait(self, t, raw=False):
        sem, val, src = t
        if src == self.name and self.compute:
            if not (raw and SAME_ENGINE_RAW_SYNC and self.name != "pe"):
                return
        k = id(sem)
        if self.seen.get(k, 0) >= val:
            return
        self.h.wait_ge(sem, val)
        self.seen[k] = val

    def deps(self, reads, writes):
        for b in reads:
            for t in b.w.values():
                self.wait(t, raw=True)
        for b in writes:
            for t in b.w.values():
                self.wait(t)
            for t in b.r.values():
                self.wait(t)

    def commit(self, t, key, reads, writes):
        for b in reads:
            b.r[key] = t
        for b in writes:
            b.w = {key: t}
            b.r = {}

    def op(self, inst, reads, writes):
        if self.count >= 30000:
            self.sem = self.ctx.new_sem()
            self.count = 0
        self.count += 1
        inst.then_inc(self.sem, 1)
        t = (self.sem, self.count, self.name)
        self.commit(t, self.name, reads, writes)
        return t

    def _slot(self, nslots=8):
        if not self.slots:
            self.slots = [[self.ctx.new_sem(), 0] for _ in range(nslots)]
        s = self.slots[self.rr]
        self.rr = (self.rr + 1) % len(self.slots)
        if s[1] >= 30000:
            s[0] = self.ctx.new_sem()
            s[1] = 0
        if s[1] > 0:
            self.wait((s[0], s[1], "dma"))
        return s

    def dma(self, out, in_, reads, writes, **kw):
        s = self._slot()
        self.deps(reads, writes)
        inst = self.h.dma_start(out=out, in_=in_, **kw)
        s[1] += 16
        inst.then_inc(s[0], 16)
        t = (s[0], s[1], "dma")
        self.ctx.dma_id += 1
        self.commit(t, "dma%d" % self.ctx.dma_id, reads, writes)
        return t

    def cc(self, ins, outs, groups, reads, writes):
        if not hasattr(self, "ccslots"):
            self.ccslots = [[self.ctx.new_sem(), 0] for _ in range(1)]
            self.ccrr = 0
        s = self.ccslots[self.ccrr]
        self.ccrr = (self.ccrr + 1) % 1
        if s[1] > 0:
            self.wait((s[0], s[1], "dma"))
        self.deps(reads, writes)
        inst = self.h.collective_compute("AllGather", ALU.bypass, replica_groups=groups, ins=ins, outs=outs)
        s[1] += 1
        inst.then_inc(s[0], 1)
        t = (s[0], s[1], "dma")
        self.ctx.dma_id += 1
        self.commit(t, "dma%d" % self.ctx.dma_id, reads, writes)
        return t


class Ctx:
    def __init__(self, nc):
        self.nc = nc
        self.nsem = 0
        self.dma_id = 0
        self.pe = Eng(self, "pe", nc.tensor, True)
        self.act = Eng(self, "act", nc.scalar, True)
        self.dve = Eng(self, "dve", nc.vector, True)
        self.pool = Eng(self, "pool", nc.gpsimd, True)
        self.sp = Eng(self, "sp", nc.sync, False)
        self.engs = [self.pe, self.act, self.dve, self.pool, self.sp]
        self.allbufs = []

    def new_sem(self):
        self.nsem += 1
        return self.nc.semaphore("s%d" % self.nsem).__enter__()

    def buf(self, name=""):
        b = Buf(name)
        self.allbufs.append(b)
        return b

    def barrier(self):
        tickets = []
        for e in self.engs:
            if e.compute and e.count > 0:
                tickets.append((e.sem, e.count, e.name))
            for s in e.slots + getattr(e, "ccslots", []):
                if s[1] > 0:
                    tickets.append((s[0], s[1], "dma"))
        for e in self.engs:
            for t in tickets:
                if t[2] == e.name:
                    continue
                e.wait(t)
        for b in self.allbufs:
            b.w = {}
            b.r = {}

    def mm(self, out, lhsT, rhs, start, stop, reads, writes):
        e = self.pe
        e.deps(reads, writes)
        inst = self.nc.tensor.matmul(out, lhsT, rhs, start=start, stop=stop)
        return e.op(inst, reads, writes)

    def actf(self, out, in_, func, reads, writes, bias=0.0, scale=1.0):
        e = self.act
        e.deps(reads, writes)
        inst = self.nc.scalar.activation(out=out, in_=in_, func=func, bias=bias, scale=scale)
        return e.op(inst, reads, writes)

    def ts(self, out, in0, s1, s2, op0, op1, reads, writes, eng=None):
        e = eng or self.dve
        e.deps(reads, writes)
        if op1 is None:
            inst = e.h.tensor_scalar(out=out, in0=in0, scalar1=s1, scalar2=None, op0=op0)
        else:
            inst = e.h.tensor_scalar(out=out, in0=in0, scalar1=s1, scalar2=s2, op0=op0, op1=op1)
        return e.op(inst, reads, writes)

    def stt(self, out, in0, scalar, in1, op0, op1, reads, writes, eng=None):
        e = eng or self.dve
        e.deps(reads, writes)
        inst = e.h.scalar_tensor_tensor(out=out, in0=in0, scalar=scalar, in1=in1, op0=op0, op1=op1)
        return e.op(inst, reads, writes)

    def tt(self, out, in0, in1, op, reads, writes, eng=None):
        e = eng or self.dve
        e.deps(reads, writes)
        inst = e.h.tensor_tensor(out=out, in0=in0, in1=in1, op=op)
        return e.op(inst, reads, writes)

    def copy(self, out, in_, reads, writes, eng=None):
        e = eng or self.dve
        e.deps(reads, writes)
        inst = e.h.tensor_copy(out=out, in_=in_)
        return e.op(inst, reads, writes)

    def recip(self, out, in_, reads, writes):
        e = self.dve
        e.deps(reads, writes)
        inst = e.h.reciprocal(out=out, in_=in_)
        return e.op(inst, reads, writes)

    def memset(self, ap, val, writes, eng=None):
        e = eng or self.dve
        e.deps([], writes)
        inst = e.h.memset(ap, val)
        return e.op(inst, [], writes)


class Rot:
    def __init__(self, items):
        self.items = items
        self.i = 0

    def next(self):
        x = self.items[self.i]
        self.i = (self.i + 1) % len(self.items)
        return x


def build_program(cfg):
    nc = bass.Bass("TRN2", target_bir_lowering=False)
    cx = Ctx(nc)
    C = cfg
    D, DC, TOK, TT, NTT, S, B, QR, KVR = C.D, C.DC, C.TOK, C.TT, C.NTT, C.S, C.B, C.QR, C.KVR
    QRC, KVC = QR // 128, KVR // 128
    NTOK = C.NTOK
    L = C.L
    G8 = [list(range(NCORES))]

    def din(name, shape, dt=F32):
        return nc.dram_tensor(name, list(shape), dt, kind="ExternalInput").ap()

    def dint(name, shape, dt):
        return nc.dram_tensor(name, list(shape), dt, kind="Internal").ap()

    xT_in = din("xT", [D, TOK])
    cT_in = din("cT", [128, DC, 2])
    pos_in = din("pos", [1, NTOK], I32)
    invf_in = din("invf", [64, 1])
    consts_in = din("cmat", [128, 4, 128], BF16)
    msb_in = din("msb", [128, 4, 512], BF16)
    mmla_in = din("mmla", [128, 4, 512], BF16)
    rotp_in = din("rotp", [64, 64])
    out_T = nc.dram_tensor("outT", [D, TOK], F32, kind="ExternalOutput").ap()

    WSPEC = {
        "wqd": (DC, QR, min(256, QR)),
        "wkvd": (DC, KVR, min(256, KVR)),
        "wkr": (DC, 64, 64),
        "wgl": (DC, 2 * D, 256),
        "wosb": (16, D, 256),
        "womla": (16, D, 256),
        "wout": (DC, D, 256),
        "wup": (DC, C.DFF, C.CTU),
        "wdown": (C.DFF // 128, D, 512),
    }
    W_in, W_bnc, W_full, W_buf = {}, {}, {}, {}
    per_l = {}
    for l in range(L):
        for nm, (KC, N, CT) in WSPEC.items():
            n_el = (KC // 8) * 128 * N
            assert n_el % 2048 == 0
            key = "%s%d" % (nm, l)
            W_in[key] = din(key, [n_el // 2048, 2048])
            W_bnc[key] = dint(key + "_b", [n_el // 2048, 2048], BF16)
            W_full[key] = dint(key + "_f", [8 * n_el // 2048, 2048], BF16)
            W_buf[key] = (cx.buf(key + "_b"), cx.buf(key + "_f"))
        per_l[l] = dict(
            wada=din("wada%d" % l, [128, DC, C.ACOLS]),
            bada=din("bada%d" % l, [1, C.ACOLS]),
            g4=din("g4_%d" % l, [128, 4, DC]),
            gq=din("gq%d" % l, [128, QRC]),
            gkv=din("gkv%d" % l, [128, KVC]),
            wqkv=din("wqkv%d" % l, [128, DC, 768]),
            wuq=din("wuq%d" % l, [128, QRC, 384]),
            wukv=din("wukv%d" % l, [128, KVC, 512]),
        )

    xres = [dint("xres%d" % i, [D, TOK], F32) for i in range(2)]
    xres_b = [cx.buf("xres%d" % i) for i in range(2)]
    gates_d = dint("gates", [2 * DC, 128, TOK], BF16)
    gates_b = cx.buf("gates")
    hT_d = dint("hT_b", [D, TOK], BF16)
    hT_db = cx.buf("hT_d")
    Hall_d = dint("Hall", [8 * D, TOK], BF16)
    Hall_b = cx.buf("Hall")
    CQW = QR + KVR + 64
    cq_d = dint("cq_b", [CQW, TOK], BF16)
    cq_db = cx.buf("cq_d")
    CQall_d = dint("CQall", [8 * CQW, TOK], BF16)
    CQall_b = cx.buf("CQall")
    osb_d = dint("osb_b", [256, NTOK], BF16)
    osb_db = cx.buf("osb_d")
    OSBall_d = dint("OSBall", [8 * 256, NTOK], BF16)
    OSBall_b = cx.buf("OSBall")
    omla_d = dint("omla_b", [256, NTOK], BF16)
    omla_db = cx.buf("omla_d")
    OMLAall_d = dint("OMLAall", [8 * 256, NTOK], BF16)
    OMLAall_b = cx.buf("OMLAall")
    ada_d = [dint("ada_b%d" % l, [2, C.ACOLS], F32) for l in range(L)]
    ada_db = [cx.buf() for l in range(L)]
    ADAall_d = [dint("ADAall%d" % l, [16, C.ACOLS], F32) for l in range(L)]
    ADAall_b = [cx.buf() for l in range(L)]

    psum = [nc.psum_tensor("ps%d" % i, [128, 512], F32).__enter__() for i in range(8)]
    psb = [cx.buf("ps%d" % i) for i in range(8)]

    sbn = [0]

    def sb(name, shape, dt):
        sbn[0] += 1
        return nc.sbuf_tensor("%s_%d" % (name, sbn[0]), list(shape), dt)

    pid = nc.gpsimd.partition_id()
    pid_sp = nc.sync.partition_id()

    cmat = sb("cmat_s", [128, 4, 128], BF16).__enter__()
    cmat_b = cx.buf("cmat")
    msb = sb("msb_s", [128, 4, 512], BF16).__enter__()
    mmla = sb("mmla_s", [128, 4, 512], BF16).__enter__()
    rotp = sb("rotp_s", [64, 64], F32).__enter__()
    cs_d = dint("cs_d", [64, 2, NTOK], F32)
    cs_db = cx.buf("cs_d")
    adaT = [sb("adaT%d" % l, [128, 6 * DC], F32).__enter__() for l in range(L)]
    adaT_b = [cx.buf() for l in range(L)]
    g4s = [sb("g4s%d" % l, [128, 4, DC], F32).__enter__() for l in range(L)]
    gqs = [sb("gqs%d" % l, [128, QRC], F32).__enter__() for l in range(L)]
    gkvs = [sb("gkvs%d" % l, [128, KVC], F32).__enter__() for l in range(L)]
    der = [sb("der%d" % l, [128, 4, DC], F32).__enter__() for l in range(L)]
    small_b = cx.buf("small")
    rstd = sb("rstd", [128, TOK], F32).__enter__()
    rstd_b = cx.buf("rstd")

    ONES = cmat[:, 0, :]
    NONES = cmat[:, 1, :]
    NUI = cmat[:, 2, :]
    IDN = cmat[:, 3, :]

    cx.sp.dma(cmat[:], consts_in, [], [cmat_b])
    cx.sp.dma(msb[:], msb_in, [], [cmat_b])
    cx.sp.dma(mmla[:], mmla_in, [], [cmat_b])
    cx.sp.dma(rotp[:], rotp_in, [], [cmat_b])
    for l in range(L):
        cx.sp.dma(g4s[l][:], per_l[l]["g4"], [], [small_b])
        cx.sp.dma(gqs[l][:], per_l[l]["gq"], [], [small_b])
        cx.sp.dma(gkvs[l][:], per_l[l]["gkv"], [], [small_b])

    def prep_weight(key):
        bb, fb = W_buf[key]
        src, bnc, full = W_in[key], W_bnc[key], W_full[key]
        rows = src.shape[0]
        step = 512
        for r0 in range(0, rows, step):
            r1 = min(rows, r0 + step)
            cx.pool.dma(bnc[r0:r1, :], src[r0:r1, :], [], [bb])
        cx.pool.cc([bnc], [full], G8, [bb], [fb])

    def wtile_src(key, j):
        KC, N, CT = WSPEC[key[:-1]]
        J = N // CT
        kk = KC // 8
        v = W_full[key].rearrange("a e -> (a e)").rearrange("(r j p x c) -> r j p x c", r=8, j=J, p=128, x=kk)
        return v[:, j].rearrange("r p x c -> p r x c")

    def phase_rope():
        CH = min(2048, NTOK)
        with sb("posi", [64, CH], I32) as posi, sb("posf", [64, CH], F32) as posf, \
                sb("invf", [64, 1], F32) as invf, sb("ang", [64, CH], F32) as ang, \
                sb("cso", [64, 2, CH], F32) as cso, sb("frc", [64, CH], F32) as frc:
            tb = cx.buf()
            cx.sp.dma(invf[:], invf_in, [], [tb])
            PI = float(np.pi)
            for c0 in range(0, NTOK, CH):
                cx.sp.dma(posi[:], pos_in[:, c0:c0 + CH].partition_broadcast(64), [], [tb])
                cx.copy(posf[:], posi[:], [tb], [tb])
                cx.ts(posf[:], posf[:], invf[:, 0:1], 1.0 / (2 * PI), ALU.mult, ALU.mult, [tb], [tb])
                for which, shift in ((1, 0.0), (0, 0.25)):
                    cx.ts(ang[:], posf[:], shift, None, ALU.add, None, [tb], [tb])
                    cx.copy(posi[:], ang[:], [tb], [tb])
                    cx.copy(frc[:], posi[:], [tb], [tb])
                    cx.tt(ang[:], ang[:], frc[:], ALU.subtract, [tb], [tb])
                    cx.ts(frc[:], ang[:], 0.5, None, ALU.is_gt, None, [tb], [tb])
                    cx.tt(ang[:], ang[:], frc[:], ALU.subtract, [tb], [tb])
                    cx.ts(frc[:], ang[:], -0.5, None, ALU.is_lt, None, [tb], [tb])
                    cx.tt(ang[:], ang[:], frc[:], ALU.add, [tb], [tb])
                    cx.actf(cso[:, which, :], ang[:], AF.Sin, [tb], [tb], scale=2 * PI)
                cx.sp.dma(cs_d[:, :, c0:c0 + CH], cso[:], [tb], [cs_db])
            cx.barrier()

    def phase_ada():
        with sb("cact", [128, DC, 2], F32) as cact, sb("adarow", [2, C.ACOLS], F32) as adarow, \
                sb("badar", [2, C.ACOLS], F32) as badar, sb("wa0", [128, 2, C.ACOLS], F32) as wa0, \
                sb("wa1", [128, 2, C.ACOLS], F32) as wa1, sb("arow", [1, 6 * D], F32) as arow, \
                sb("one1", [1, 1], F32) as one1:
            tb = cx.buf()
            cx.sp.dma(cact[:], cT_in, [], [tb])
            cx.actf(cact[:], cact[:], AF.Silu, [tb], [tb])
            cx.memset(one1[:], 1.0, [tb])
            wab = [cx.buf(), cx.buf()]
            was = [wa0, wa1]
            cts_ = [(c0, min(512, C.ACOLS - c0)) for c0 in range(0, C.ACOLS, 512)]
            for l in range(L):
                for k2 in range(DC // 2):
                    w = was[k2 % 2]
                    cx.sp.dma(w[:], per_l[l]["wada"][:, 2 * k2:2 * k2 + 2, :], [], [wab[k2 % 2]])
                    for kk in range(2):
                        k = 2 * k2 + kk
                        for ct, (c0, cw) in enumerate(cts_):
                            cx.mm(psum[ct][0:2, 0:cw], cact[:, k, :], w[:, kk, c0:c0 + cw],
                                  k == 0, k == DC - 1, [tb, wab[k2 % 2]], [psb[ct]])
                rb = cx.buf()
                cx.sp.dma(badar[0:1, :], per_l[l]["bada"], [], [rb])
                cx.sp.dma(badar[1:2, :], per_l[l]["bada"], [], [rb])
                for ct, (c0, cw) in enumerate(cts_):
                    cx.tt(adarow[:, c0:c0 + cw], psum[ct][0:2, 0:cw], badar[:, c0:c0 + cw],
                          ALU.add, [psb[ct], rb], [rb])
                cx.pool.dma(ada_d[l], adarow[:], [rb], [ada_db[l]])
                cx.pool.cc([ada_d[l]], [ADAall_d[l]], G8, [ada_db[l]], [ADAall_b[l]])
                bsel = pid // 4
                src = ADAall_d[l].rearrange("(r b) j -> b r j", b=2)[bass.ds(bsel, 1)]
                cx.pool.dma(arow[:].rearrange("o (r j) -> o r j", r=8), src, [ADAall_b[l]], [rb])
                for ch in range(6 * DC):
                    cx.mm(psum[6][:, ch:ch + 1], arow[0:1, ch * 128:(ch + 1) * 128], one1[:], True, True,
                          [rb], [psb[6]])
                cx.copy(adaT[l][:], psum[6][:, 0:6 * DC], [psb[6]], [adaT_b[l]])
                dl, g = der[l], g4s[l]
                A = adaT[l]
                rd = [adaT_b[l], small_b]
                cx.stt(dl[:, 0, :], A[:, DC:2 * DC], 1.0, g[:, 0, :], ALU.add, ALU.mult, rd, [small_b])
                cx.tt(dl[:, 1, :], A[:, 2 * DC:3 * DC], g[:, 1, :], ALU.mult, rd, [small_b])
                cx.stt(dl[:, 2, :], A[:, 4 * DC:5 * DC], 1.0, g[:, 2, :], ALU.add, ALU.mult, rd, [small_b])
                cx.tt(dl[:, 3, :], A[:, 5 * DC:6 * DC], g[:, 3, :], ALU.mult, rd, [small_b])
            cx.barrier()

    def ss_accumulate(src_ap, src_bufs, sq, sq_b, k, nk, tok0, ntok, banks):
        cx.actf(sq[:, 0:ntok], src_ap, AF.Square, src_bufs, [sq_b])
        nb = (ntok + 511) // 512
        for n in range(nb):
            w = min(512, ntok - n * 512)
            cx.mm(psum[banks[n]][:, 0:w], ONES, sq[:, n * 512:n * 512 + w], k == 0, k == nk - 1,
                  [cmat_b, sq_b], [psb[banks[n]]])

    def finish_rstd(banks, tok0, ntok, nfeat):
        nb = (ntok + 511) // 512
        for n in range(nb):
            w = min(512, ntok - n * 512)
            sl = rstd[:, tok0 + n * 512: tok0 + n * 512 + w]
            cx.actf(sl, psum[banks[n]][:, 0:w], AF.Sqrt, [psb[banks[n]]], [rstd_b], bias=EPS_AP[:, 0:1],
                    scale=1.0 / nfeat)
            cx.recip(sl, sl, [rstd_b], [rstd_b])

    epsc = sb("epsc", [128, 1], F32).__enter__()
    EPS_AP = epsc
    cx.memset(epsc[:], EPS, [small_b])

    def phase_prenorm(l, which, xsrc, xsrc_b, hT, hT_b, tok0, ntok):
        a_idx = 0 if which == 0 else 2
        sh_off = 0 if which == 0 else 3 * DC
        with sb("xk0", [128, ntok], F32) as xk0, sb("xk1", [128, ntok], F32) as xk1, \
                sb("sq0", [128, ntok], BF16) as sq0, sb("sq1", [128, ntok], BF16) as sq1:
            xks, xkb = [xk0, xk1], [cx.buf(), cx.buf()]
            sqs, sqb = [sq0, sq1], [cx.buf(), cx.buf()]
            banks = [6, 7]
            for k in range(DC):
                cx.sp.dma(xks[k % 2][:], xsrc[k * 128:(k + 1) * 128, tok0:tok0 + ntok], [xsrc_b], [xkb[k % 2]])
                ss_accumulate(xks[k % 2][:], [xkb[k % 2]], sqs[k % 2], sqb[k % 2], k, DC, 0, ntok, banks)
            finish_rstd(banks, tok0, ntok, D)
            for k in range(DC):
                cx.sp.dma(xks[k % 2][:], xsrc[k * 128:(k + 1) * 128, tok0:tok0 + ntok], [xsrc_b], [xkb[k % 2]])
                cx.stt(xks[k % 2][:], xks[k % 2][:], der[l][:, a_idx, k:k + 1], rstd[:, tok0:tok0 + ntok],
                       ALU.mult, ALU.mult, [xkb[k % 2], small_b, rstd_b], [xkb[k % 2]])
                cx.actf(hT[:, k, 0:ntok], xks[k % 2][:], AF.Identity, [xkb[k % 2], adaT_b[l]], [hT_b],
                        bias=adaT[l][:, sh_off + k: sh_off + k + 1], scale=1.0)
            cx.barrier()

    def postnorm_tile(l, which, yacc, yacc_b, n, xsrc, xsrc_b, xdst, xdst_b):
        c_idx = 1 if which == 0 else 3
        t0 = n * TT
        with sb("pn_sq0", [128, TT], BF16) as sq0, sb("pn_sq1", [128, TT], BF16) as sq1, \
                sb("pn_x0", [128, TT], F32) as x0, sb("pn_x1", [128, TT], F32) as x1:
            sqs, sqb = [sq0, sq1], [cx.buf(), cx.buf()]
            xs, xb = [x0, x1], [cx.buf(), cx.buf()]
            for k in range(DC):
                ss_accumulate(yacc[:, k, :], [yacc_b], sqs[k % 2], sqb[k % 2], k, DC, 0, TT, [7])
            finish_rstd([7], t0, TT, D)
            for k in range(DC):
                cx.sp.dma(xs[k % 2][:], xsrc[k * 128:(k + 1) * 128, t0:t0 + TT], [xsrc_b], [xb[k % 2]])
                cx.stt(yacc[:, k, :], yacc[:, k, :], der[l][:, c_idx, k:k + 1], rstd[:, t0:t0 + TT],
                       ALU.mult, ALU.mult, [yacc_b, small_b, rstd_b], [yacc_b])
                cx.tt(xs[k % 2][:], xs[k % 2][:], yacc[:, k, :], ALU.add, [xb[k % 2], yacc_b], [xb[k % 2]])
                cx.sp.dma(xdst[k * 128:(k + 1) * 128, t0:t0 + TT], xs[k % 2][:], [xb[k % 2]], [xdst_b])
            cx.barrier()

    def dense_ws(key, rhs_fn, rhs_bufs, ntt, tt, epi, banks, wt_tiles, wt_bufs, jrange=None, kc_sub=None):
        KC, N, CT = WSPEC[key[:-1]]
        J = N // CT
        js = list(range(J)) if jrange is None else list(jrange)
        fb = W_buf[key][1]
        rot = Rot(banks)
        kcs = list(range(KC)) if kc_sub is None else kc_sub

        def load(ji):
            j = js[ji]
            wt, wb = wt_tiles[ji % 2], wt_bufs[ji % 2]
            src = wtile_src(key, j)
            if CT == wt.shape[2]:
                dst = wt[:, 0:KC, :].rearrange("p (r x) c -> p r (x c)", r=8)
                src = src.rearrange("p r x c -> p r (x c)")
            else:
                kk_ = KC // 8
                for r_ in range(8):
                    cx.sp.dma(wt[:, r_ * kk_:(r_ + 1) * kk_, 0:CT], src[:, r_], [fb], [wb])
                return
            cx.sp.dma(dst, src, [fb], [wb])

        load(0)
        for ji, j in enumerate(js):
            if ji + 1 < len(js):
                load(ji + 1)
            wt, wb = wt_tiles[ji % 2], wt_bufs[ji % 2]
            nmb = (CT + 127) // 128
            for mb in range(nmb):
                mw = min(128, CT - mb * 128)
                for n in range(ntt):
                    bk = rot.next()
                    for ki, k in enumerate(kcs):
                        cx.mm(psum[bk][0:mw, 0:tt], wt[:, k, mb * 128:mb * 128 + mw], rhs_fn(k, n),
                              ki == 0, ki == len(kcs) - 1, [wb] + rhs_bufs, [psb[bk]])
                    epi(j * CT + mb * 128, mw, n, bk)

    def phase_tokproj(l, hT, hT_b):
        with sb("wt0", [128, DC, 256], BF16) as wt0, sb("wt1", [128, DC, 256], BF16) as wt1, \
                sb("qd", [128, QRC, TOK], F32) as qd, sb("kvd", [128, KVC + 1, TOK], F32) as kvd, \
                sb("gt0", [128, TOK], BF16) as gt0, sb("gt1", [128, TOK], BF16) as gt1, \
                sb("sq0c", [128, TOK], BF16) as sq0, sb("cqo", [128, TOK], BF16) as cqo, \
                sb("cqo1", [128, TOK], BF16) as cqo1, sb("rt", [64, TOK], F32) as rt, \
                sb("csl", [64, 2, TOK], F32) as csl:
            wts, wbs = [wt0, wt1], [cx.buf(), cx.buf()]
            qd_b, kvd_b = cx.buf(), cx.buf()
            rhs = lambda k, n: hT[:, k, n * TT:(n + 1) * TT]
            evr = Rot([0, 1])

            def ev_copy(dst, dst_b, bk, mw):
                if evr.next() == 0:
                    cx.actf(dst, psum[bk][0:mw, 0:TT], AF.Identity, [psb[bk]], [dst_b])
                else:
                    cx.copy(dst, psum[bk][0:mw, 0:TT], [psb[bk]], [dst_b])

            def epi_qd(col, mw, n, bk):
                ev_copy(qd[0:mw, col // 128, n * TT:(n + 1) * TT], qd_b, bk, mw)

            def epi_kvd(col, mw, n, bk):
                ev_copy(kvd[0:mw, col // 128, n * TT:(n + 1) * TT], kvd_b, bk, mw)

            def epi_kr(col, mw, n, bk):
                ev_copy(kvd[0:mw, KVC, n * TT:(n + 1) * TT], kvd_b, bk, mw)

            gts, gtb = [gt0, gt1], [cx.buf(), cx.buf()]
            gcount = [0]

            def epi_gl(col, mw, n, bk):
                blk = col // 128
                i = gcount[0] % 2
                cx.actf(gts[i][:, n * TT:(n + 1) * TT], psum[bk][:, 0:TT], AF.Sigmoid, [psb[bk]], [gtb[i]])
                if n == NTT - 1:
                    cx.sp.dma(gates_d[blk], gts[i][:], [gtb[i]], [gates_b])
                    gcount[0] += 1

            B4 = [0, 1, 2, 3]
            dense_ws("wqd%d" % l, rhs, [hT_b], NTT, TT, epi_qd, B4, wts, wbs)
            dense_ws("wkvd%d" % l, rhs, [hT_b], NTT, TT, epi_kvd, B4, wts, wbs)
            dense_ws("wkr%d" % l, rhs, [hT_b], NTT, TT, epi_kr, B4, wts, wbs)
            sqb = cx.buf()
            cqs, cqb = [cqo, cqo1], [cx.buf(), cx.buf()]
            for (src, srcb, nch, gs, row0) in ((qd, qd_b, QRC, gqs[l], 0), (kvd, kvd_b, KVC, gkvs[l], QR)):
                for k in range(nch):
                    ss_accumulate(src[:, k, :], [srcb], sq0, sqb, k, nch, 0, TOK, [6, 7])
                finish_rstd([6, 7], 0, TOK, nch * 128)
                for k in range(nch):
                    i = k % 2
                    cx.stt(cqs[i][:], src[:, k, :], gs[:, k:k + 1], rstd[:], ALU.mult, ALU.mult,
                           [srcb, small_b, rstd_b], [cqb[i]])
                    cx.sp.dma(cq_d[row0 + k * 128: row0 + (k + 1) * 128, :], cqs[i][:], [cqb[i]], [cq_db])
            kr = kvd[0:64, KVC, :]
            rtb = cx.buf()
            cslb = cx.buf()
            tk0 = pid_sp * TOK
            cx.sp.dma(csl[:], cs_d[:, :, bass.ds(tk0, TOK)], [cs_db], [cslb])
            for n in range(NTT):
                sl = slice(n * TT, (n + 1) * TT)
                cx.mm(psum[4][0:64, 0:TT], rotp[:], kr[:, sl], True, True, [cmat_b, kvd_b], [psb[4]])
                cx.tt(rt[:, sl], psum[4][0:64, 0:TT], csl[:, 1, sl], ALU.mult, [psb[4], cslb], [rtb])
                cx.tt(kr[:, sl], kr[:, sl], csl[:, 0, sl], ALU.mult, [kvd_b, cslb], [kvd_b])
                cx.tt(cqo[0:64, sl], kr[:, sl], rt[:, sl], ALU.add, [kvd_b, rtb], [cqb[0]])
            cx.sp.dma(cq_d[QR + KVR: QR + KVR + 64, :], cqo[0:64, :], [cqb[0]], [cq_db])
            cx.pool.cc([cq_d], [CQall_d], G8, [cq_db], [CQall_b])
            dense_ws("wgl%d" % l, rhs, [hT_b], NTT, TT, epi_gl, B4, wts, wbs)
            cx.barrier()

    def attn_tmp():
        ts_ = [sb("at_e0", [128, 512], F32), sb("at_e1", [128, 512], F32), sb("at_sp0", [128, 512], BF16),
               sb("at_sp1", [128, 512], BF16), sb("at_sp2", [128, 512], BF16), sb("at_S", [128, 512], BF16),
               sb("at_w0", [128, 512], BF16), sb("at_w1", [128, 512], BF16), sb("at_w2", [128, 512], BF16),
               sb("at_rc", [128, 512], F32)]
        tiles_ = tuple(t.__enter__() for t in ts_)
        bufs_ = (cx.buf(), cx.buf(), [cx.buf(), cx.buf()], [cx.buf(), cx.buf(), cx.buf()],
                 [cx.buf(), cx.buf(), cx.buf()])
        return ts_, (tiles_, bufs_)

    def attn_tmp_free(ts_):
        for t in reversed(ts_):
            t.__exit__(None, None, None)

    def attention_core(kind, qT, kT, vv, qrT, krT, bufs_in, oT, oT_b, tmp):
        NQT = S // 512
        tiles = []
        for i in range(NQT):
            nkb = 4 * i + 4
            for jj, j in enumerate(range(nkb - 1, -1, -1)):
                tiles.append((i, j, jj == 0, jj == nkb - 1, j >= 4 * i))
        T = len(tiles)
        if True:
            (e0, e1, sp0, sp1, sp2, Sacc, w0, w1, w2, rc), (Sb, rcb, ebs, spb, wb) = tmp
            es = [e0, e1]
            sps = [sp0, sp1, sp2]
            ws = [w0, w1, w2]
            obank = lambda i: 4 + (i % 2)
            dbank = lambda i: 6 + (i % 2)

            def stA(t):
                i, j, first, last, diag = tiles[t]
                pb = t % 4
                qs = slice(i * 512, (i + 1) * 512)
                ks = slice(j * 128, (j + 1) * 128)
                if kind == "sb":
                    cx.mm(psum[pb][:, :], kT[:, ks], qT[:, qs], True, not diag, bufs_in, [psb[pb]])
                    if diag:
                        cx.mm(psum[pb][:, :], IDN, msb[:, j - 4 * i, :], False, True, [cmat_b], [psb[pb]])
                else:
                    cx.mm(psum[pb][:, :], kT[:, ks], qT[:, qs], True, False, bufs_in, [psb[pb]])
                    cx.mm(psum[pb][:, :], krT[:, ks], qrT[:, qs], False, not diag, bufs_in, [psb[pb]])
                    if diag:
                        cx.mm(psum[pb][:, :], IDN, mmla[:, j - 4 * i, :], False, True, [cmat_b], [psb[pb]])

            def stB(t):
                pb = t % 4
                if kind == "sb":
                    cx.actf(es[t % 2][:], psum[pb][:, :], AF.Exp, [psb[pb]], [ebs[t % 2]])
                    cx.actf(sps[t % 3][:], es[t % 2][:], AF.Ln, [ebs[t % 2]], [spb[t % 3]], bias=1.0, scale=1.0)
                else:
                    cx.actf(ws[t % 3][:], psum[pb][:, :], AF.Exp, [psb[pb]], [wb[t % 3]])

            def stC(t):
                i, j, first, last, diag = tiles[t]
                pb = t % 4
                c = t % 3
                cx.mm(psum[pb][:, :], NUI, sps[c][:], False, first, [cmat_b, spb[c]], [psb[pb]])
                if not first:
                    cx.mm(psum[pb][:, :], NONES, Sacc[:], False, True, [cmat_b, Sb], [psb[pb]])
                if not last:
                    if first:
                        cx.copy(Sacc[:], sps[c][:], [spb[c]], [Sb])
                    else:
                        cx.tt(Sacc[:], Sacc[:], sps[c][:], ALU.add, [Sb, spb[c]], [Sb])

            def stD(t):
                pb = t % 4
                cx.actf(ws[t % 3][:], psum[pb][:, :], AF.Exp, [psb[pb]], [wb[t % 3]])

            def stE(t):
                i, j, first, last, diag = tiles[t]
                c = t % 3
                ob = obank(i)
                qs = slice(i * 512, (i + 1) * 512)
                cx.mm(psum[ob][:, :], vv[:, j, :], ws[c][:], first, last, bufs_in + [wb[c]], [psb[ob]])
                if kind == "mla":
                    db = dbank(i)
                    cx.mm(psum[db][:, :], ONES, ws[c][:], first, last, [cmat_b, wb[c]], [psb[db]])
                if last:
                    if kind == "sb":
                        cx.copy(oT[:, qs], psum[ob][:, :], [psb[ob]], [oT_b])
                    else:
                        cx.recip(rc[:], psum[db][:, :], [psb[db]], [rcb])
                        cx.tt(oT[:, qs], psum[ob][:, :], rc[:], ALU.mult, [psb[ob], rcb], [oT_b])

            if kind == "sb":
                for s_ in range(T + 2):
                    if s_ < T:
                        stA(s_)
                        stB(s_)
                    if 0 <= s_ - 1 < T:
                        stC(s_ - 1)
                        stD(s_ - 1)
                    if 0 <= s_ - 2 < T:
                        stE(s_ - 2)
            else:
                for s_ in range(T + 1):
                    if s_ < T:
                        stA(s_)
                        stB(s_)
                    if 0 <= s_ - 1 < T:
                        stE(s_ - 1)

    def phase_sb(l, after_w=None):
        wq = per_l[l]["wqkv"]
        with sb("wqkv_s", [128, DC, 768], BF16) as wqkv, sb("ht0", [128, DC, 512], BF16) as ht0, \
                sb("ht1", [128, DC, 512], BF16) as ht1, sb("qTs", [128, 2, S], BF16) as qTs, \
                sb("kTs", [128, 2, S], BF16) as kTs, sb("vs", [128, S // 128, 256], BF16) as vs, \
                sb("oTs", [128, 2, S], BF16) as oTs:
            wqb = cx.buf()
            at_cm, at_tmp = attn_tmp()
            for k0 in range(0, DC, 8):
                cx.pool.dma(wqkv[:, k0:k0 + 8, :], wq[:, k0:k0 + 8, :], [], [wqb])
            if after_w is not None:
                after_w()
            hts, htb = [ht0, ht1], [cx.buf(), cx.buf()]
            qkvb, ob = cx.buf(), cx.buf()
            scale = 128 ** -0.5
            TPB = S // 512
            for b in range(B):
                for t in range(TPB):
                    g = b * S + t * 512
                    r, off = g // TOK, g % TOK
                    i = t % 2
                    wdt = min(512, TOK)
                    for sub in range(512 // wdt):
                        gg = g + sub * wdt
                        r, off = gg // TOK, gg % TOK
                        src = Hall_d[r * D:(r + 1) * D, off:off + wdt].rearrange("(k p) t -> p k t", p=128)
                        cx.sp.dma(hts[i][:, :, sub * wdt:(sub + 1) * wdt], src, [Hall_b], [htb[i]])
                    ts_ = slice(t * 512, (t + 1) * 512)
                    for hd in range(2):
                        for which, dstT in ((0, qTs), (1, kTs)):
                            bk = [0, 1, 2, 3][(2 * hd + which) % 4]
                            c0 = which * 256 + hd * 128
                            for k in range(DC):
                                cx.mm(psum[bk][:, :], wqkv[:, k, c0:c0 + 128], hts[i][:, k, :], k == 0, k == DC - 1,
                                      [wqb, htb[i]], [psb[bk]])
                            if which == 0:
                                cx.actf(dstT[:, hd, ts_], psum[bk][:, :], AF.Identity, [psb[bk]], [qkvb], scale=scale)
                            else:
                                cx.copy(dstT[:, hd, ts_], psum[bk][:, :], [psb[bk]], [qkvb])
                    for s4 in range(4):
                        bk = 4 + (s4 % 2)
                        for k in range(DC):
                            cx.mm(psum[bk][:, 0:256], hts[i][:, k, s4 * 128:(s4 + 1) * 128], wqkv[:, k, 512:768],
                                  k == 0, k == DC - 1, [wqb, htb[i]], [psb[bk]])
                        cx.copy(vs[:, t * 4 + s4, :], psum[bk][:, 0:256], [psb[bk]], [qkvb])
                for hd in range(2):
                    attention_core("sb", qTs[:, hd, :], kTs[:, hd, :], vs[:, :, hd * 128:(hd + 1) * 128], None, None,
                                   [qkvb], oTs[:, hd, :], ob, at_tmp)
                for hd in range(2):
                    cx.sp.dma(osb_d[hd * 128:(hd + 1) * 128, b * S:(b + 1) * S], oTs[:, hd, :], [ob], [osb_db])
            cx.pool.cc([osb_d], [OSBall_d], G8, [osb_db], [OSBall_b])
            cx.barrier()
            attn_tmp_free(at_cm)

    def phase_mla(l, after_w=None):
        with sb("wuq_s", [128, QRC, 384], BF16) as wuq, sb("wukv_s", [128, KVC, 512], BF16) as wukv, \
                sb("cqt0", [128, QRC + KVC + 1, 512], BF16) as cqt0, sb("cqt1", [128, QRC + KVC + 1, 512], BF16) as cqt1, \
                sb("qnT", [128, 2, S], BF16) as qnT, sb("knT", [128, 2, S], BF16) as knT, \
                sb("qrT", [64, 2, S], BF16) as qrT, sb("krT", [64, S], BF16) as krT, \
                sb("vm", [128, S // 128, 256], BF16) as vm, sb("oTm", [128, 2, S], BF16) as oTm, \
                sb("qrf", [64, 512], F32) as qrf, sb("qrt", [64, 512], F32) as qrt, \
                sb("cslm0", [64, 2, 512], F32) as cslm0, sb("cslm1", [64, 2, 512], F32) as cslm1:
            wb_ = cx.buf()
            at_cm, at_tmp = attn_tmp()
            csls, cslbs = [cslm0, cslm1], [cx.buf(), cx.buf()]
            cx.pool.dma(wuq[:], per_l[l]["wuq"], [], [wb_])
            cx.pool.dma(wukv[:], per_l[l]["wukv"], [], [wb_])
            if after_w is not None:
                after_w()
            cts, ctb = [cqt0, cqt1], [cx.buf(), cx.buf()]
            pb_, ob = cx.buf(), cx.buf()
            qrfb = cx.buf()
            scale = (128 + 64) ** -0.5
            TPB = S // 512
            for b in range(B):
                for t in range(TPB):
                    g = b * S + t * 512
                    i = t % 2
                    wdt = min(512, TOK)
                    for sub in range(512 // wdt):
                        gg = g + sub * wdt
                        r, off = gg // TOK, gg % TOK
                        base = r * CQW
                        src = CQall_d[base:base + QR + KVR, off:off + wdt].rearrange("(k p) t -> p k t", p=128)
                        cx.sp.dma(cts[i][:, 0:QRC + KVC, sub * wdt:(sub + 1) * wdt], src, [CQall_b], [ctb[i]])
                        cx.sp.dma(cts[i][0:64, QRC + KVC, sub * wdt:(sub + 1) * wdt],
                                  CQall_d[base + QR + KVR: base + QR + KVR + 64, off:off + wdt], [CQall_b], [ctb[i]])
                    ts_ = slice(t * 512, (t + 1) * 512)
                    gs_ = slice(g, g + 512)
                    ct = cts[i]
                    cx.sp.dma(csls[i][:], cs_d[:, :, g:g + 512], [cs_db], [cslbs[i]])
                    cx.copy(krT[:, ts_], ct[0:64, QRC + KVC, :], [ctb[i]], [pb_])
                    for hd in range(2):
                        bk = hd
                        for k in range(QRC):
                            cx.mm(psum[bk][:, :], wuq[:, k, hd * 128:(hd + 1) * 128], ct[:, k, :], k == 0, k == QRC - 1,
                                  [wb_, ctb[i]], [psb[bk]])
                        cx.actf(qnT[:, hd, ts_], psum[bk][:, :], AF.Identity, [psb[bk]], [pb_], scale=scale)
                        bk = 2 + hd
                        for k in range(QRC):
                            cx.mm(psum[bk][0:64, :], wuq[:, k, 256 + hd * 64:256 + (hd + 1) * 64], ct[:, k, :], k == 0,
                                  k == QRC - 1, [wb_, ctb[i]], [psb[bk]])
                        cx.actf(qrf[:], psum[bk][0:64, :], AF.Identity, [psb[bk]], [qrfb], scale=scale)
                        cx.mm(psum[bk][0:64, :], rotp[:], qrf[:], True, True, [cmat_b, qrfb], [psb[bk]])
                        cx.tt(qrt[:], psum[bk][0:64, :], csls[i][:, 1, :], ALU.mult, [psb[bk], cslbs[i]], [qrfb])
                        cx.tt(qrf[:], qrf[:], csls[i][:, 0, :], ALU.mult, [qrfb, cslbs[i]], [qrfb])
                        cx.tt(qrT[:, hd, ts_], qrf[:], qrt[:], ALU.add, [qrfb], [pb_])
                        bk = 4 + hd
                        for k in range(KVC):
                            cx.mm(psum[bk][:, :], wukv[:, k, hd * 128:(hd + 1) * 128], ct[:, QRC + k, :], k == 0,
                                  k == KVC - 1, [wb_, ctb[i]], [psb[bk]])
                        cx.copy(knT[:, hd, ts_], psum[bk][:, :], [psb[bk]], [pb_])
                    for s4 in range(4):
                        bk = 6 + (s4 % 2)
                        for k in range(KVC):
                            cx.mm(psum[bk][:, 0:256], ct[:, QRC + k, s4 * 128:(s4 + 1) * 128], wukv[:, k, 256:512],
                                  k == 0, k == KVC - 1, [wb_, ctb[i]], [psb[bk]])
                        cx.copy(vm[:, t * 4 + s4, :], psum[bk][:, 0:256], [psb[bk]], [pb_])
                for hd in range(2):
                    attention_core("mla", qnT[:, hd, :], knT[:, hd, :], vm[:, :, hd * 128:(hd + 1) * 128],
                                   qrT[:, hd, :], krT[:, :], [pb_], oTm[:, hd, :], ob, at_tmp)
                for hd in range(2):
                    cx.sp.dma(omla_d[hd * 128:(hd + 1) * 128, b * S:(b + 1) * S], oTm[:, hd, :], [ob], [omla_db])
            cx.pool.cc([omla_d], [OMLAall_d], G8, [omla_db], [OMLAall_b])
            cx.barrier()
            attn_tmp_free(at_cm)

    def phase_merge_out(l, mT, mT_b, xsrc, xsrc_b, xdst, xdst_b):
        tk0 = pid_sp * TOK
        with sb("osbT", [128, 16, TOK], BF16) as osbT, sb("omlT", [128, 16, TOK], BF16) as omlT, \
                sb("wo0", [128, 16, 256], BF16) as wo0, sb("wo1", [128, 16, 256], BF16) as wo1, \
                sb("wm0", [128, 16, 256], BF16) as wm0, sb("wm1", [128, 16, 256], BF16) as wm1, \
                sb("gs0", [128, TOK], BF16) as gs0, sb("gs1", [128, TOK], BF16) as gs1, \
                sb("gm0", [128, TOK], BF16) as gm0, sb("gm1", [128, TOK], BF16) as gm1, \
                sb("tmpa", [128, TT], F32) as tmpa, sb("tmpb", [128, TT], F32) as tmpb:
            ib = cx.buf()
            cx.sp.dma(osbT[:], OSBall_d[:, bass.ds(tk0, TOK)].rearrange("(k p) t -> p k t", p=128),
                      [OSBall_b], [ib])
            cx.sp.dma(omlT[:], OMLAall_d[:, bass.ds(tk0, TOK)].rearrange("(k p) t -> p k t", p=128),
                      [OMLAall_b], [ib])
            wos, wob = [wo0, wo1], [cx.buf(), cx.buf()]
            wms, wmb = [wm0, wm1], [cx.buf(), cx.buf()]
            gss, gsb = [gs0, gs1], [cx.buf(), cx.buf()]
            gms, gmb = [gm0, gm1], [cx.buf(), cx.buf()]
            tmps, tmpb_ = [tmpa, tmpb], [cx.buf(), cx.buf()]
            ksb, kml = "wosb%d" % l, "womla%d" % l
            KC, N, CT = WSPEC["wosb"]
            J = N // CT

            def load(j):
                for key, wt, wb in ((ksb, wos[j % 2], wob[j % 2]), (kml, wms[j % 2], wmb[j % 2])):
                    src = wtile_src(key, j).rearrange("p r x c -> p r (x c)")
                    cx.sp.dma(wt[:].rearrange("p (r x) c -> p r (x c)", r=8), src, [W_buf[key][1]], [wb])

            load(0)
            cnt = 0
            for j in range(J):
                if j + 1 < J:
                    load(j + 1)
                for mb in range(CT // 128):
                    blk = (j * CT) // 128 + mb
                    gi = blk % 2
                    cx.sp.dma(gss[gi][:], gates_d[blk], [gates_b], [gsb[gi]])
                    cx.sp.dma(gms[gi][:], gates_d[DC + blk], [gates_b], [gmb[gi]])
                    for n in range(NTT):
                        tsl = slice(n * TT, (n + 1) * TT)
                        ba, bb_ = (0, 1) if cnt % 2 == 0 else (2, 3)
                        ti = cnt % 2
                        cnt += 1
                        for k in range(16):
                            cx.mm(psum[ba][:, 0:TT], wos[j % 2][:, k, mb * 128:(mb + 1) * 128], osbT[:, k, tsl],
                                  k == 0, k == 15, [wob[j % 2], ib], [psb[ba]])
                        for k in range(16):
                            cx.mm(psum[bb_][:, 0:TT], wms[j % 2][:, k, mb * 128:(mb + 1) * 128], omlT[:, k, tsl],
                                  k == 0, k == 15, [wmb[j % 2], ib], [psb[bb_]])
                        cx.tt(tmps[ti][:], psum[ba][:, 0:TT], gss[gi][:, tsl], ALU.mult, [psb[ba], gsb[gi]],
                              [tmpb_[ti]])
                        cx.tt(gms[gi][:, tsl], psum[bb_][:, 0:TT], gms[gi][:, tsl], ALU.mult, [psb[bb_], gmb[gi]],
                              [gmb[gi]])
                        cx.tt(mT[:, blk, tsl], tmps[ti][:], gms[gi][:, tsl], ALU.add, [tmpb_[ti], gmb[gi]], [mT_b],
                              eng=cx.pool)
            cx.barrier()
        with sb("wu0", [128, DC, 256], BF16) as wu0, sb("wu1", [128, DC, 256], BF16) as wu1, \
                sb("yacc", [128, DC, TT], F32) as yacc:
            wts, wbs = [wu0, wu1], [cx.buf(), cx.buf()]
            yb = cx.buf()
            for n in range(NTT):
                tsl = slice(n * TT, (n + 1) * TT)

                def epi_y(col, mw, nn, bk):
                    cx.actf(yacc[:, col // 128, :], psum[bk][:, 0:TT], AF.Identity, [psb[bk]], [yb])

                dense_ws("wout%d" % l, lambda k, nn: mT[:, k, tsl], [mT_b], 1, TT, epi_y, [0, 1, 2, 3], wts, wbs)
                postnorm_tile(l, 0, yacc, yb, n, xsrc, xsrc_b, xdst, xdst_b)

    def phase_ffn(l, xsrc, xsrc_b, xdst, xdst_b):
        HBC, CTU = C.HBC, C.CTU
        for n in range(NTT):
            tsl = slice(n * TT, (n + 1) * TT)
            with sb("h2T", [128, DC, TT], BF16) as hT:
                hT_b = cx.buf()
                phase_prenorm(l, 1, xsrc, xsrc_b, hT, hT_b, n * TT, TT)
                with sb("fu0", [128, DC, CTU], BF16) as fu0, sb("fu1", [128, DC, CTU], BF16) as fu1, \
                        sb("fd0", [128, HBC, 512], BF16) as fd0, sb("fd1", [128, HBC, 512], BF16) as fd1, \
                        sb("uT", [128, HBC, TT], BF16) as uT, sb("yaccf", [128, DC, TT], F32) as yacc, \
                        sb("usq0", [128, TT], F32) as usq0, sb("usq1", [128, TT], F32) as usq1:
                    wus, wub = [fu0, fu1], [cx.buf(), cx.buf()]
                    wds, wdb = [fd0, fd1], [cx.buf(), cx.buf()]
                    ub, yb = cx.buf(), cx.buf()
                    usq, usqb = [usq0, usq1], [cx.buf(), cx.buf()]
                    ucnt = [0]
                    for hb in range(8):
                        def epi_u(col, mw, nn, bk):
                            kc = (col - hb * C.HB) // 128
                            i = ucnt[0] % 2
                            ucnt[0] += 1
                            cx.actf(usq[i][:], psum[bk][:, 0:TT], AF.Square, [psb[bk]], [usqb[i]])
                            cx.stt(uT[:, kc, :], psum[bk][:, 0:TT], 0.0, usq[i][:], ALU.is_gt, ALU.mult,
                                   [psb[bk], usqb[i]], [ub])

                        jt = range(hb * (C.HB // CTU), (hb + 1) * (C.HB // CTU))
                        dense_ws("wup%d" % l, lambda k, nn: hT[:, k, :], [hT_b], 1, TT, epi_u, [0, 1, 2, 3], wus, wub,
                                 jrange=jt)

                        def epi_d(col, mw, nn, bk):
                            blk = col // 128
                            if hb == 0:
                                cx.actf(yacc[:, blk, :], psum[bk][:, 0:TT], AF.Identity, [psb[bk]], [yb])
                            else:
                                cx.tt(yacc[:, blk, :], yacc[:, blk, :], psum[bk][:, 0:TT], ALU.add, [psb[bk], yb],
                                      [yb])

                        dense_down(l, hb, uT, ub, epi_d, wds, wdb)
                    postnorm_tile(l, 1, yacc, yb, n, xsrc, xsrc_b, xdst, xdst_b)

    def dense_down(l, hb, uT, ub, epi, wds, wdb):
        key = "wdown%d" % l
        KC, N, CT = WSPEC["wdown"]
        J = N // CT
        HBC = C.HBC
        fb = W_buf[key][1]
        rot = Rot([4, 5, 6])
        v = W_full[key].rearrange("(r j p x) e -> r j p (x e)", r=8, j=J, p=128)

        def load(j):
            cx.sp.dma(wds[j % 2][:].rearrange("p k c -> p (k c)"), v[hb, j], [fb], [wdb[j % 2]])

        load(0)
        for j in range(J):
            if j + 1 < J:
                load(j + 1)
            wt, wb = wds[j % 2], wdb[j % 2]
            for mb in range(CT // 128):
                bk = rot.next()
                for k in range(HBC):
                    cx.mm(psum[bk][:, 0:TT], wt[:, k, mb * 128:(mb + 1) * 128], uT[:, k, :], k == 0, k == HBC - 1,
                          [wb, ub], [psb[bk]])
                epi(j * CT + mb * 128, 128, 0, bk)

    order0 = ["wqd", "wkvd", "wkr", "wgl"]
    order1 = ["wosb", "womla", "wout", "wup", "wdown"]
    cx.barrier()
    phase_rope()
    for nm in order0:
        prep_weight(nm + "0")
    phase_ada()
    xcur, xcur_b = xT_in, cx.buf("xin")
    for l in range(L):
        with sb("hT_%d" % l, [128, DC, TOK], BF16) as hT:
            hT_b = cx.buf()
            phase_prenorm(l, 0, xcur, xcur_b, hT, hT_b, 0, TOK)
            for k in range(DC):
                cx.sp.dma(hT_d[k * 128:(k + 1) * 128, :], hT[:, k, :], [hT_b], [hT_db])
            cx.pool.cc([hT_d], [Hall_d], G8, [hT_db], [Hall_b])
            phase_tokproj(l, hT, hT_b)
        def prep_a(l=l):
            for nm in ("wosb", "womla", "wout"):
                prep_weight(nm + str(l))

        def prep_b(l=l):
            prep_weight("wup%d" % l)

        phase_sb(l, after_w=prep_a)
        phase_mla(l, after_w=prep_b)
        prep_weight("wdown%d" % l)
        x1, x1_b = xres[0], xres_b[0]
        with sb("mT_%d" % l, [128, DC, TOK], BF16) as mT:
            mT_b = cx.buf()
            phase_merge_out(l, mT, mT_b, xcur, xcur_b, x1, x1_b)
        if l + 1 < L:
            for nm in order0:
                prep_weight(nm + str(l + 1))
        last = l == L - 1
        x2, x2_b = (out_T, cx.buf("out")) if last else (xres[1], xres_b[1])
        phase_ffn(l, x1, x1_b, x2, x2_b)
        xcur, xcur_b = x2, x2_b
    cx.barrier()
    return nc


def _arr_ws(W, KC, N, CT, c):
    kk = KC // 8
    J = N // CT
    Wc = W[c * kk * 128:(c + 1) * kk * 128, :]
    a = Wc.reshape(kk, 128, J, CT).transpose(2, 1, 0, 3)
    return np.ascontiguousarray(a).reshape(-1, 2048)


def make_consts():
    bf = ml_dtypes.bfloat16
    cm = np.zeros((128, 4, 128), np.float32)
    cm[:, 0, :] = 1.0
    cm[:, 1, :] = -1.0
    si = np.arange(128)[:, None]
    so = np.arange(128)[None, :]
    cm[:, 2, :] = np.where(si >= so, -1.0, 0.0)
    cm[:, 3, :] = np.eye(128)
    msb = np.zeros((128, 4, 512), np.float32)
    mml = np.zeros((128, 4, 512), np.float32)
    s = np.arange(128)[:, None]
    t = np.arange(512)[None, :]
    for jj in range(4):
        key = 128 * jj + s
        msb[:, jj, :] = np.where(key < t, 0.0, NEG)
        mml[:, jj, :] = np.where((key // 64) <= (t // 64), 0.0, NEG)
    rp = np.zeros((64, 64), np.float32)
    for i in range(32):
        rp[i + 32, i] = -1.0
        rp[i, i + 32] = 1.0
    half = 32
    invf = (10000.0 ** (-np.arange(half, dtype=np.float32) / half)).astype(np.float32)
    invf2 = np.concatenate([invf, invf]).reshape(64, 1).astype(np.float32)
    return cm.astype(bf), msb.astype(bf), mml.astype(bf), rp, invf2


def make_in_maps(cfg, x, c, positions, w_ada, b_ada, g_pre_mix, g_post_mix, g_pre_mlp, g_post_mlp, w_in,
                 g_q_lora, w_uq, g_kv_lora, w_ukv, w_o_sb, w_o_mla, w_out, w_up, w_down):
    C = cfg
    D, DC, TOK, QR, KVR = C.D, C.DC, C.TOK, C.QR, C.KVR
    cm, msb, mml, rp, invf2 = make_consts()
    f32 = lambda a: np.ascontiguousarray(np.asarray(a, dtype=np.float32))
    x = np.asarray(x)
    xf = x.reshape(C.B * C.S, D)
    pos = np.ascontiguousarray(np.asarray(positions).reshape(1, -1).astype(np.int32))
    cT = f32(np.asarray(c).T.reshape(DC, 128, 2).transpose(1, 0, 2))
    o_q, o_k, o_v = 0, 2048, 4096
    o_qd = 6144
    o_kvd = o_qd + QR
    o_gl = o_kvd + KVR + 64
    maps = []
    for cc in range(NCORES):
        m = {}
        m["xT"] = f32(xf[cc * TOK:(cc + 1) * TOK, :].T)
        m["cT"] = cT
        m["pos"] = pos
        m["invf"] = invf2
        m["cmat"] = cm
        m["msb"] = msb
        m["mmla"] = mml
        m["rotp"] = rp
        for l in range(C.L):
            win = np.asarray(w_in[l])
            ws = {
                "wqd": win[:, o_qd:o_qd + QR],
                "wkvd": win[:, o_kvd:o_kvd + KVR],
                "wkr": win[:, o_kvd + KVR:o_kvd + KVR + 64],
                "wgl": win[:, o_gl:o_gl + 2 * D],
                "wosb": np.asarray(w_o_sb[l]),
                "womla": np.asarray(w_o_mla[l]),
                "wout": np.asarray(w_out[l]),
                "wup": np.asarray(w_up[l]),
                "wdown": np.asarray(w_down[l]),
            }
            spec = {
                "wqd": (DC, QR, min(256, QR)), "wkvd": (DC, KVR, min(256, KVR)), "wkr": (DC, 64, 64),
                "wgl": (DC, 2 * D, 256),
                "wosb": (16, D, 256), "womla": (16, D, 256), "wout": (DC, D, 256),
                "wup": (DC, C.DFF, C.CTU), "wdown": (C.DFF // 128, D, 512),
            }
            for nm, W in ws.items():
                KC, N, CT = spec[nm]
                m["%s%d" % (nm, l)] = _arr_ws(W, KC, N, CT, cc)
            wa = np.asarray(w_ada[l])[:, cc * C.ACOLS:(cc + 1) * C.ACOLS]
            m["wada%d" % l] = f32(wa.reshape(DC, 128, C.ACOLS).transpose(1, 0, 2))
            m["bada%d" % l] = f32(np.asarray(b_ada[l])[cc * C.ACOLS:(cc + 1) * C.ACOLS].reshape(1, -1))
            g4 = np.stack([np.asarray(g)[l].reshape(DC, 128).T for g in
                           (g_pre_mix, g_post_mix, g_pre_mlp, g_post_mlp)], axis=1)
            m["g4_%d" % l] = f32(g4)
            m["gq%d" % l] = f32(np.asarray(g_q_lora[l]).reshape(QR // 128, 128).T)
            m["gkv%d" % l] = f32(np.asarray(g_kv_lora[l]).reshape(KVR // 128, 128).T)
            h0, h1 = 2 * cc, 2 * cc + 1
            cols = []
            for base in (o_q, o_k, o_v):
                for h in (h0, h1):
                    cols.append(win[:, base + h * 128: base + (h + 1) * 128])
            wqkv = np.concatenate(cols, axis=1)
            m["wqkv%d" % l] = f32(wqkv.reshape(DC, 128, 768).transpose(1, 0, 2))
            wq = np.asarray(w_uq[l])
            cols = [wq[:, h * 192: h * 192 + 128] for h in (h0, h1)] + \
                   [wq[:, h * 192 + 128: (h + 1) * 192] for h in (h0, h1)]
            m["wuq%d" % l] = f32(np.concatenate(cols, axis=1).reshape(QR // 128, 128, 384).transpose(1, 0, 2))
            wk = np.asarray(w_ukv[l])
            cols = [wk[:, h * 256: h * 256 + 128] for h in (h0, h1)] + \
                   [wk[:, h * 256 + 128: (h + 1) * 256] for h in (h0, h1)]
            m["wukv%d" % l] = f32(np.concatenate(cols, axis=1).reshape(KVR // 128, 128, 512).transpose(1, 0, 2))
        maps.append(m)
    return maps


_CACHE = {}


def run(cfg, **inputs):
    key = (cfg.D, cfg.S, cfg.QR, cfg.KVR)
    if key not in _CACHE:
        _CACHE[key] = build_program(cfg)
    nc = _CACHE[key]
    maps = make_in_maps(cfg, **inputs)
    res = run_bass_kernel_spmd(nc, maps, core_ids=list(range(NCORES)))
    outs = [np.asarray(res.results[c]["outT"]).T for c in range(NCORES)]
    full = np.concatenate(outs, axis=0).reshape(cfg.B, cfg.S, cfg.D).astype(np.float32)
    return full


def kernel(**inputs):
    return run(Cfg(), **inputs)
```
